# Optimizing a Trainium2 kernel written in Bass

```python
import jax, jax.numpy as jnp
from jax import lax
import numpy as np

D_MODEL = 2048
BATCH = 2
SEQ = 16384
DEPTH = 2

PLE_DIM = 256
D_FF = 5632
CONV_WIDTH = 512
CONV_K = 3
ATTN_HEADS = 16
ATTN_KV_HEADS = 2
HEAD_DIM = 64
WINDOW = 128
ATTN_BLOCK = 128
HGRN_HEADS = 8
HGRN_DK = 128
HGRN_DV = 64
HGRN_CHUNK = 64
NORM_EPS = 1e-6
NEG_BIG = -1e30
LB_FLOOR = 1e-20

IN_SPLITS = (CONV_WIDTH, CONV_WIDTH, CONV_WIDTH,
             ATTN_HEADS * HEAD_DIM, ATTN_KV_HEADS * HEAD_DIM, ATTN_KV_HEADS * HEAD_DIM,
             HGRN_HEADS * HGRN_DK, HGRN_HEADS * HGRN_DK, HGRN_HEADS * HGRN_DV, HGRN_HEADS * HGRN_DV,
             D_MODEL, D_MODEL, D_MODEL)
D_IN = sum(IN_SPLITS)

kernel_name = "hybrid_conv_swa_hgrn2_macaron_block"


def rmsnorm(x, w):
    xf = x.astype(jnp.float32)
    y = xf * lax.rsqrt(jnp.mean(xf * xf, axis=-1, keepdims=True) + NORM_EPS) * w.astype(jnp.float32)
    return y.astype(x.dtype)


def swiglu_half_step(x, norm_pre, w_gu, w_down, norm_post):
    h = rmsnorm(x, norm_pre)
    g, u = jnp.split(h @ w_gu, 2, axis=-1)
    y = (jax.nn.silu(g) * u) @ w_down
    return x + 0.5 * rmsnorm(y, norm_post)


def short_conv(xin, b_gate, c_gate, conv_w):
    u = c_gate * xin
    s = u.shape[1]
    y = conv_w[CONV_K - 1] * u
    for j in range(1, CONV_K):
        y = y + conv_w[CONV_K - 1 - j] * jnp.pad(u, ((0, 0), (j, 0), (0, 0)))[:, :s]
    return b_gate * y


def alibi_slopes(n_heads):
    return jnp.exp2(-8.0 * jnp.arange(1, n_heads + 1, dtype=jnp.float32) / n_heads)


def swa_attention(q, k, v, sinks):
    dtype = q.dtype
    b, s, _ = q.shape
    nb = s // ATTN_BLOCK
    g = ATTN_HEADS // ATTN_KV_HEADS
    qb = q.astype(jnp.float32).reshape(b, nb, ATTN_BLOCK, ATTN_KV_HEADS, g, HEAD_DIM)
    kb = k.astype(jnp.float32).reshape(b, nb, ATTN_BLOCK, ATTN_KV_HEADS, HEAD_DIM)
    vb = v.astype(jnp.float32).reshape(b, nb, ATTN_BLOCK, ATTN_KV_HEADS, HEAD_DIM)

    def with_prev(t):
        prev = jnp.concatenate([jnp.zeros_like(t[:, :1]), t[:, :-1]], axis=1)
        return jnp.concatenate([prev, t], axis=2)

    kk, vv = with_prev(kb), with_prev(vb)
    scores = jnp.einsum('bnqhgd,bnkhd->bnhgqk', qb, kk) * (HEAD_DIM ** -0.5)
    qi = jnp.arange(ATTN_BLOCK)
    ki = jnp.arange(2 * ATTN_BLOCK)
    dist = qi[:, None] + ATTN_BLOCK - ki[None, :]
    band = (dist >= 0) & (dist < WINDOW)
    key_pos = jnp.arange(nb)[:, None] * ATTN_BLOCK - ATTN_BLOCK + ki[None, :]
    mask = band[None] & (key_pos >= 0)[:, None, :]
    slopes = alibi_slopes(ATTN_HEADS).reshape(ATTN_KV_HEADS, g)
    scores = scores - slopes[:, :, None, None] * dist.astype(jnp.float32)
    scores = jnp.where(mask[None, :, None, None], scores, NEG_BIG)
    sink = sinks.astype(jnp.float32).reshape(ATTN_KV_HEADS, g)[:, :, None]
    m = jnp.maximum(scores.max(axis=-1), sink)
    pr = jnp.exp(scores - m[..., None])
    denom = pr.sum(axis=-1) + jnp.exp(sink - m)
    o = jnp.einsum('bnhgqk,bnkhd->bnqhgd', pr / denom[..., None], vv)
    return o.reshape(b, s, ATTN_HEADS * HEAD_DIM).astype(dtype)


def hgrn2(q, f_logit, i_in, g_out, lb, norm_w):
    dtype = q.dtype
    b, s, _ = q.shape
    nc = s // HGRN_CHUNK

    def to_chunks(t, d):
        t = t.astype(jnp.float32).reshape(b, nc, HGRN_CHUNK, HGRN_HEADS, d)
        return jnp.transpose(t, (1, 0, 3, 2, 4))

    qc = to_chunks(jax.nn.silu(q), HGRN_DK)
    z = to_chunks(f_logit, HGRN_DK)
    vc = to_chunks(i_in, HGRN_DV)
    lbh = lb.astype(jnp.float32).reshape(HGRN_HEADS, 1, HGRN_DK)
    log_lb = jnp.log(jnp.maximum(lbh, LB_FLOOR))
    log_f = jnp.logaddexp(jax.nn.log_sigmoid(z), log_lb + jax.nn.log_sigmoid(-z))
    kc = (1.0 - lbh) * jax.nn.sigmoid(-z)
    causal = jnp.tril(jnp.ones((HGRN_CHUNK, HGRN_CHUNK), dtype=bool))

    def step(state, xs):
        q_t, lf_t, k_t, v_t = xs
        a = jnp.cumsum(lf_t, axis=-2)
        o_inter = jnp.einsum('bhtk,bhkv->bhtv', q_t * jnp.exp(a), state)
        dec = jnp.where(causal[:, :, None], a[..., :, None, :] - a[..., None, :, :], NEG_BIG)
        att = jnp.einsum('bhtk,bhtsk,bhsk->bhts', q_t, jnp.exp(dec), k_t)
        o = o_inter + jnp.einsum('bhts,bhsv->bhtv', att, v_t)
        a_last = a[..., -1:, :]
        new_state = jnp.exp(a_last[..., 0, :])[..., None] * state + jnp.einsum(
            'bhsk,bhsv->bhkv', k_t * jnp.exp(a_last - a), v_t)
        return new_state, o

    s0 = jnp.zeros((b, HGRN_HEADS, HGRN_DK, HGRN_DV), jnp.float32)
    _, o = lax.scan(step, s0, (qc, log_f, kc, vc))
    o = jnp.transpose(o, (1, 0, 3, 2, 4)).reshape(b, s, HGRN_HEADS, HGRN_DV).astype(dtype)
    o = rmsnorm(o, norm_w).reshape(b, s, HGRN_HEADS * HGRN_DV)
    return o * jax.nn.silu(g_out)


def setup_inputs(seed: int = 0) -> dict:
    key = jax.random.key(seed)
    ks = jax.random.split(key, 25)
    f32 = jnp.float32

    def nrm(k, shape, scale):
        return scale * jax.random.normal(k, shape, f32)

    def gain(k, n):
        return 1.0 + 0.05 * jax.random.normal(k, (DEPTH, n), f32)

    return {
        "x": nrm(ks[0], (BATCH, SEQ, D_MODEL), 1.0),
        "p": nrm(ks[1], (DEPTH, BATCH, SEQ, PLE_DIM), 1.0),
        "ffn1_norm_pre": gain(ks[2], D_MODEL),
        "ffn1_w_gu": nrm(ks[3], (DEPTH, D_MODEL, 2 * D_FF), D_MODEL ** -0.5),
        "ffn1_w_down": nrm(ks[4], (DEPTH, D_FF, D_MODEL), D_FF ** -0.5),
        "ffn1_norm_post": gain(ks[5], D_MODEL),
        "mix_norm_pre": gain(ks[6], D_MODEL),
        "w_in": nrm(ks[7], (DEPTH, D_MODEL, D_IN), D_MODEL ** -0.5),
        "conv_w": nrm(ks[8], (DEPTH, CONV_K, CONV_WIDTH), CONV_K ** -0.5),
        "attn_sinks": nrm(ks[9], (DEPTH, ATTN_HEADS), 0.5),
        "hgrn_lb_logits": nrm(ks[10], (DEPTH, HGRN_HEADS * HGRN_DK), 0.5),
        "hgrn_norm": gain(ks[11], HGRN_DV),
        "w_branch_conv": nrm(ks[12], (DEPTH, CONV_WIDTH, D_MODEL), CONV_WIDTH ** -0.5),
        "w_branch_attn": nrm(ks[13], (DEPTH, ATTN_HEADS * HEAD_DIM, D_MODEL), (ATTN_HEADS * HEAD_DIM) ** -0.5),
        "w_branch_hgrn": nrm(ks[14], (DEPTH, HGRN_HEADS * HGRN_DV, D_MODEL), (HGRN_HEADS * HGRN_DV) ** -0.5),
        "w_o": nrm(ks[15], (DEPTH, D_MODEL, D_MODEL), D_MODEL ** -0.5),
        "mix_norm_post": gain(ks[16], D_MODEL),
        "ffn2_norm_pre": gain(ks[17], D_MODEL),
        "ffn2_w_gu": nrm(ks[18], (DEPTH, D_MODEL, 2 * D_FF), D_MODEL ** -0.5),
        "ffn2_w_down": nrm(ks[19], (DEPTH, D_FF, D_MODEL), D_FF ** -0.5),
        "ffn2_norm_post": gain(ks[20], D_MODEL),
        "ple_norm_pre": gain(ks[21], D_MODEL),
        "w_ple_gate": nrm(ks[22], (DEPTH, D_MODEL, D_MODEL), D_MODEL ** -0.5),
        "w_ple_proj": nrm(ks[23], (DEPTH, PLE_DIM, D_MODEL), PLE_DIM ** -0.5),
        "ple_norm_post": gain(ks[24], D_MODEL),
    }


def reference(x, p, ffn1_norm_pre, ffn1_w_gu, ffn1_w_down, ffn1_norm_post,
              mix_norm_pre, w_in, conv_w, attn_sinks, hgrn_lb_logits, hgrn_norm,
              w_branch_conv, w_branch_attn, w_branch_hgrn, w_o, mix_norm_post,
              ffn2_norm_pre, ffn2_w_gu, ffn2_w_down, ffn2_norm_post,
              ple_norm_pre, w_ple_gate, w_ple_proj, ple_norm_post):
    lb_p = jax.nn.softmax(hgrn_lb_logits.astype(jnp.float32), axis=0)
    lb_all = jnp.cumsum(lb_p, axis=0) - lb_p
    offsets = [int(o) for o in np.cumsum(IN_SPLITS)[:-1]]
    for l in range(DEPTH):
        x = swiglu_half_step(x, ffn1_norm_pre[l], ffn1_w_gu[l], ffn1_w_down[l], ffn1_norm_post[l])
        h = rmsnorm(x, mix_norm_pre[l])
        (c_x, c_b, c_c, a_q, a_k, a_v, r_q, r_f, r_i, r_g,
         g_conv, g_attn, g_hgrn) = jnp.split(h @ w_in[l], offsets, axis=-1)
        y_conv = short_conv(c_x, c_b, c_c, conv_w[l]) @ w_branch_conv[l]
        y_attn = swa_attention(a_q, a_k, a_v, attn_sinks[l]) @ w_branch_attn[l]
        y_hgrn = hgrn2(r_q, r_f, r_i, r_g, lb_all[l], hgrn_norm[l]) @ w_branch_hgrn[l]
        merged = (jax.nn.sigmoid(g_conv) * y_conv + jax.nn.sigmoid(g_attn) * y_attn
                  + jax.nn.sigmoid(g_hgrn) * y_hgrn)
        x = x + rmsnorm(merged @ w_o[l], mix_norm_post[l])
        x = swiglu_half_step(x, ffn2_norm_pre[l], ffn2_w_gu[l], ffn2_w_down[l], ffn2_norm_post[l])
        hp = rmsnorm(x, ple_norm_pre[l])
        ple = jax.nn.sigmoid(hp @ w_ple_gate[l]) * (p[l] @ w_ple_proj[l])
        x = x + rmsnorm(ple, ple_norm_post[l])
    return x
```

```python
import numpy as np
from contextlib import ExitStack
import concourse.bass as bass
import concourse.mybir as mybir
from concourse.bass_utils import run_bass_kernel_spmd

F32 = mybir.dt.float32
BF16 = mybir.dt.bfloat16
AF = mybir.ActivationFunctionType
ALU = mybir.AluOpType
AX = mybir.AxisListType

D = 2048
KC = 16
DFF = 5632
FC = 44
PLE = 256
T = 512
EPS = 1e-6
BIG = 1.0e9
O_CX, O_CB, O_CC = 0, 512, 1024
O_AQ, O_AK, O_AV = 1536, 2560, 2688
O_RQ, O_RF, O_RI, O_RG = 2816, 3840, 4864, 5376
O_GC, O_GA, O_GH = 5888, 7936, 9984
NSLOT = 4
SLOT_ELEMS = 4096
G_F1PRE, G_F1POST, G_MPRE, G_MPOST, G_F2PRE, G_F2POST, G_PPRE, G_PPOST = range(8)
P_CONV = 128
P_SINK = 140
P_LB = 156
P_HN = 164
P_W = 168
C_ID, C_BD, C_DD, C_D0, C_CM, C_SM, C_M, C_M1, C_S = 0, 128, 256, 512, 768, 1280, 1792, 1800, 1808
C_W = 1816
NX = 912


class Sched:
    ENGS = ("pe", "act", "dve", "pool", "sp")

    def __init__(self):
        self.ops = []
        self.last_w = {}
        self.readers = {}
        self.dma_keys = []

    def add(self, eng, fn, reads=(), writes=(), dma=None, inc=16):
        i = len(self.ops)
        deps = set()
        for k in reads:
            w = self.last_w.get(k)
            if w is not None:
                deps.add(w)
        for k in writes:
            w = self.last_w.get(k)
            if w is not None:
                deps.add(w)
            for r in self.readers.get(k, ()):
                deps.add(r)
        for k in reads:
            self.readers.setdefault(k, []).append(i)
        for k in writes:
            self.last_w[k] = i
            self.readers[k] = []
        if dma is not None and dma not in self.dma_keys:
            self.dma_keys.append(dma)
        self.ops.append([eng, fn, deps, dma, inc])
        return i

    def emit(self, block, sems):
        ops = self.ops
        n = len(ops)
        for i, (eng, fn, deps, dma, inc) in enumerate(ops):
            if eng == "pe" and dma is None:
                ops[i][2] = {d for d in deps if not (ops[d][0] == "pe" and ops[d][3] is None)}
        for i, op in enumerate(ops):
            best = {}
            keep = set()
            for d in op[2]:
                if ops[d][3] is not None:
                    keep.add(d)
                else:
                    e = ops[d][0]
                    if d > best.get(e, -1):
                        best[e] = d
            keep.update(best.values())
            op[2] = keep
        needed = [False] * n
        for op in ops:
            for d in op[2]:
                needed[d] = True
        tok = [None] * n
        cnt = {e: 0 for e in self.ENGS}
        dcnt = {k: 0 for k in self.dma_keys}
        dma_before = [None] * n
        for i, (eng, fn, deps, dma, inc) in enumerate(ops):
            if deps and any(ops[d][3] is not None for d in deps):
                dma_before[i] = dict(dcnt)
            if dma is not None:
                dcnt[dma] += inc
                tok[i] = ("d_" + str(dma), None)
            elif needed[i]:
                cnt[eng] += 1
                tok[i] = ("e_" + eng, cnt[eng])
        self.stats = dict(cnt=cnt, nops=n, maxd=max(dcnt.values()) if dcnt else 0)
        per_eng = {e: [] for e in self.ENGS}
        for i, op in enumerate(ops):
            per_eng[op[0]].append(i)
        final_d = dict(dcnt)

        def run(engname, engobj):
            seen = {}
            for i in per_eng[engname]:
                eng, fn, deps, dma, inc = ops[i]
                waits = {}
                for d in deps:
                    s, v = tok[d]
                    if v is None:
                        v = dma_before[i][ops[d][3]]
                    if v > waits.get(s, 0):
                        waits[s] = v
                for s, v in waits.items():
                    if v > seen.get(s, 0):
                        engobj.wait_ge(sems[s], v)
                        seen[s] = v
                ins = fn(engobj)
                if dma is not None:
                    ins.then_inc(sems["d_" + str(dma)], inc)
                elif needed[i]:
                    ins.then_inc(sems["e_" + engname], 1)
            if engname == "sp":
                for k, v in final_d.items():
                    if v > 0:
                        engobj.wait_ge(sems["d_" + str(k)], v)

        @block.tensor
        def _(e):
            run("pe", e)

        @block.scalar
        def _(e):
            run("act", e)

        @block.vector
        def _(e):
            run("dve", e)

        @block.gpsimd
        def _(e):
            run("pool", e)

        @block.sync
        def _(e):
            run("sp", e)


class WRef:
    def __init__(self, name):
        self.name = name

    def __getitem__(self, l):
        return (self.name, l)


WSHAPES = {"ffn1_w_gu": (D, 2 * DFF), "ffn1_w_down": (DFF, D), "w_in": (D, 12032), "w_branch_conv": (512, D),
           "w_branch_attn": (1024, D), "w_branch_hgrn": (512, D), "w_o": (D, D), "ffn2_w_gu": (D, 2 * DFF),
           "ffn2_w_down": (DFF, D), "w_ple_gate": (D, D), "w_ple_proj": (PLE, D)}
WORDER_A = ["ffn1_w_gu", "ffn1_w_down", "w_in"]
WORDER_B = ["w_branch_conv", "w_branch_attn", "w_branch_hgrn", "w_o", "ffn2_w_gu", "ffn2_w_down", "w_ple_gate", "w_ple_proj"]


class Gen:
    def __init__(self, seg, depth, parts):
        self.seg = seg
        self.depth = depth
        self.nt = seg // T
        self.parts = parts
        self.wblocks = []
        self.dry = True

    def op(self, eng, fn, reads=(), writes=(), dma=None, inc=16):
        if not self.dry:
            self.S.add(eng, fn, reads, writes, dma, inc)

    def mm(self, out, lhsT, rhs, start, stop, reads, writes):
        self.op("pe", lambda e: e.matmul(out, lhsT=lhsT, rhs=rhs, start=start, stop=stop), reads, writes)

    def tr(self, out, in_, ident, reads, writes):
        self.op("pe", lambda e: e.transpose(out=out, in_=in_, identity=ident), reads, writes)

    def act(self, out, in_, func, reads, writes, bias=None, scale=None, accum=None):
        kw = {}
        if bias is not None:
            kw["bias"] = bias
        if scale is not None:
            kw["scale"] = scale
        if accum is not None:
            kw["accum_out"] = accum
        self.op("act", lambda e: e.activation(out=out, in_=in_, func=func, **kw), reads, writes)

    def tt(self, eng, out, in0, in1, op, reads, writes):
        self.op(eng, lambda e: e.tensor_tensor(out=out, in0=in0, in1=in1, op=op), reads, writes)

    def ts(self, eng, out, in0, s1, s2, op0, op1, reads, writes):
        if s2 is None:
            self.op(eng, lambda e: e.tensor_scalar(out=out, in0=in0, scalar1=s1, scalar2=None, op0=op0), reads, writes)
        else:
            self.op(eng, lambda e: e.tensor_scalar(out=out, in0=in0, scalar1=s1, scalar2=s2, op0=op0, op1=op1), reads, writes)

    def stt(self, eng, out, in0, scalar, in1, op0, op1, reads, writes):
        self.op("dve", lambda e: e.scalar_tensor_tensor(out=out, in0=in0, scalar=scalar, in1=in1, op0=op0, op1=op1), reads, writes)

    def cp(self, eng, out, in_, reads, writes):
        if eng == "act":
            self.op("act", lambda e: e.activation(out=out, in_=in_, func=AF.Copy), reads, writes)
        else:
            self.op(eng, lambda e: e.tensor_copy(out=out, in_=in_), reads, writes)

    def dma(self, eng, out, in_, reads, writes, key):
        self.op(eng, lambda e: e.dma_start(out=out, in_=in_), reads, writes, dma=key)

    def bank(self):
        i = self.bank_i
        self.bank_i = (i + 1) % 8
        return i

    def ev_eng(self):
        self.ev_i ^= 1
        return "act" if self.ev_i else "dve"

    def wload(self, W, k0, nk, pieces, gw):
        if self.dry:
            self.wblocks.append((W, k0, nk, pieces, gw))
            return 0
        while self.wl < min(self.wi + NSLOT, len(self.wblocks)):
            Wb, bk0, bnk, bp, bgw = self.wblocks[self.wl]
            slot = self.wl % NSLOT
            wname, wl_ = Wb
            Wd = self.wbf[wname][wl_]
            for (dc, sc, w) in bp:
                dst = self.WS[slot][:, 0:bnk * bgw].rearrange("p (k m) -> p k m", m=bgw)[:, :, dc:dc + w]
                src = Wd[bk0 * 128:(bk0 + bnk) * 128, sc:sc + w].rearrange("(k p) m -> p k m", p=128)
                self.dma("sp", dst, src, [("WB", wname, wl_)], [("WS", slot)], "ws%d" % slot)
            self.wl += 1
        slot = self.wi % NSLOT
        self.wi += 1
        return slot

    def wap(self, slot, nk, gw):
        return self.WS[slot][:, 0:nk * gw].rearrange("p (k m) -> p k m", m=gw)

    def mlin(self, specs, ncols, evac, n=T):
        for g0 in range(0, ncols, 256):
            gw = min(256, ncols - g0)
            nch = gw // 128
            banks = [[self.bank() for _ in range(nch)] for _ in specs]
            for si, (W, kch, col0, rhs) in enumerate(specs):
                for k0 in range(0, kch, 16):
                    nk = min(16, kch - k0)
                    slot = self.wload(W, k0, nk, [(0, col0 + g0, gw)], gw)
                    wa = self.wap(slot, nk, gw)
                    for ci in range(nch):
                        b = banks[si][ci]
                        for kk in range(nk):
                            kc = k0 + kk
                            rap, rkeys = rhs(kc)
                            self.mm(self.PS[b][:, 0:n], wa[:, kk, ci * 128:(ci + 1) * 128], rap,
                                    kc == 0, kc == kch - 1, [("WS", slot)] + rkeys, [("ps", b)])
            for ci in range(nch):
                evac(g0 // 128 + ci, [banks[si][ci] for si in range(len(specs))])

    def ykeys(self, lo, hi):
        return [("Y", c) for c in range(lo // T, (hi - 1) // T + 1)]

    def gain(self, l, g):
        return self.PRM[:, l * P_W + g * 16: l * P_W + (g + 1) * 16]

    def rms_stats(self, src, skeys, ncomb=KC, lhs=None):
        b = self.bank()
        for c in range(ncomb):
            s = self.sq_i % 4
            self.sq_i += 1
            sq = self.ABF[:, s, :]
            self.act(sq, src(c), AF.Square, skeys(c), [("AB", 2 * s), ("AB", 2 * s + 1)])
            self.mm(self.PS[b][:, 0:T], self.ONES[:, :], sq, c == 0, c == ncomb - 1,
                    [("AB", 2 * s), ("AB", 2 * s + 1), "ONES"], [("ps", b)])
        self.act(self.RS[:, :], self.PS[b][:, 0:T], AF.Sqrt, [("ps", b), "EPSC"], ["RS"], bias=self.EPSC[:, 0:1])
        self.op("dve", lambda e: e.reciprocal(out=self.RS[:, :], in_=self.RS[:, :]), ["RS"], ["RS"])

    def prenorm(self, l, g):
        self.rms_stats(lambda c: self.X[:, c, :], lambda c: [("X", c)])
        gn = self.gain(l, g)
        for c in range(KC):
            self.stt("dve", self.H[:, c, :], self.X[:, c, :], gn[:, c:c + 1], self.RS[:, :], ALU.mult, ALU.mult,
                     [("X", c), "RS", "PRM"], [("H", c)])

    def postnorm_add(self, l, g, half):
        self.rms_stats(lambda c: self.Y[:, c, :], lambda c: [("Y", c)])
        gn = (self.GH if half else self.PRM)[:, l * P_W + g * 16: l * P_W + (g + 1) * 16]
        for c in range(KC):
            self.stt("pool", self.Y[:, c, :], self.Y[:, c, :], gn[:, c:c + 1], self.RS[:, :], ALU.mult, ALU.mult,
                     [("Y", c), "RS", "PRM", "GH"], [("Y", c)])
            self.tt("dve", self.X[:, c, :], self.X[:, c, :], self.Y[:, c, :], ALU.add, [("X", c), ("Y", c)], [("X", c)])

    def ffn(self, l, w_gu, w_down, gpre, gpost):
        self.prenorm(l, gpre)
        Wgu = w_gu[l]
        Wd = w_down[l]
        hr = lambda kc: (self.H[:, kc, :], [("H", kc)])

        def ev_up(j, bs):
            s = self.tmp_i % 4
            self.tmp_i += 1
            self.act(self.TMP[:, s, :], self.PS[bs[0]][:, 0:T], AF.Silu, [("ps", bs[0])], [("TMP", s)])
            self.tt("dve", self.AB[:, j, :], self.TMP[:, s, :], self.PS[bs[1]][:, 0:T], ALU.mult,
                    [("TMP", s), ("ps", bs[1])], [("AB", j)])
        self.mlin([(Wgu, KC, 0, hr), (Wgu, KC, DFF, hr)], DFF, ev_up)

        def ev_dn(m, bs):
            self.cp(self.ev_eng(), self.Y[:, m, :], self.PS[bs[0]][:, 0:T], [("ps", bs[0])], [("Y", m)])
        self.mlin([(Wd, FC, 0, lambda kc: (self.AB[:, kc, :], [("AB", kc)]))], D, ev_dn)
        self.postnorm_add(l, gpost, True)

    def ple(self, l, t):
        for tb in range(4):
            st = self.TMP[:, tb, 0:PLE]
            self.dma("sp", st, self.p_d[l, t * T + tb * 128: t * T + (tb + 1) * 128, :], [], [("TMP", tb)], "pld")
        b = self.bank()
        for j in range(2):
            for tb in range(4):
                self.tr(self.PS[b][:, tb * 128:(tb + 1) * 128], self.TMP[:, tb, j * 128:(j + 1) * 128], self.IDF[:, :],
                        [("TMP", tb), "CST"], [("ps", b)])
            self.cp("act", self.PT[:, j, :], self.PS[b][:, 0:T], [("ps", b)], [("PT", j)])
            if j == 0:
                b = self.bank()
        self.prenorm(l, G_PPRE)

        def ev(m, bs):
            s = self.tmp_i % 4
            self.tmp_i += 1
            self.act(self.TMP[:, s, :], self.PS[bs[0]][:, 0:T], AF.Sigmoid, [("ps", bs[0])], [("TMP", s)])
            self.tt("dve", self.Y[:, m, :], self.TMP[:, s, :], self.PS[bs[1]][:, 0:T], ALU.mult,
                    [("TMP", s), ("ps", bs[1])], [("Y", m)])
        self.mlin([(self.w_ple_gate[l], KC, 0, lambda kc: (self.H[:, kc, :], [("H", kc)])),
                   (self.w_ple_proj[l], 2, 0, lambda kc: (self.PT[:, kc, :], [("PT", kc)]))], D, ev)
        self.postnorm_add(l, G_PPOST, False)

    def conv_u(self, l):
        Win = self.w_in[l]
        hr = lambda kc: (self.H[:, kc, :], [("H", kc)])
        uk = self.ykeys(0, 2056)
        self.cp("pool", self.UE[:, :, 0:2], self.UC[l][:, :, :], ["UC%d" % l], uk)

        def ev(j, bs):
            s = self.tmp_i % 4
            self.tmp_i += 1
            self.cp("act", self.TMP[:, s, :], self.PS[bs[0]][:, 0:T], [("ps", bs[0])], [("TMP", s)])
            self.tt("dve", self.UE[:, j, 2:514], self.TMP[:, s, :], self.PS[bs[1]][:, 0:T], ALU.mult,
                    [("TMP", s), ("ps", bs[1])], uk)
        self.mlin([(Win, KC, O_CX, hr), (Win, KC, O_CC, hr)], 512, ev)

    def conv_carry(self, l):
        self.cp("pool", self.UC[l][:, :, :], self.UE[:, :, 512:514], self.ykeys(0, 2056), ["UC%d" % l])

    def conv_out(self, l):
        Win = self.w_in[l]
        hr = lambda kc: (self.H[:, kc, :], [("H", kc)])
        uk = self.ykeys(0, 2056)
        cw = self.PRM[:, l * P_W + P_CONV: l * P_W + P_CONV + 12]

        def ev(j, bs):
            s = self.tmp_i % 4
            self.tmp_i += 1
            t1 = self.TMP[:, s, :]
            self.ts("dve", t1, self.UE[:, j, 2:514], cw[:, j * 3 + 2: j * 3 + 3], None, ALU.mult, None, uk + ["PRM"], [("TMP", s)])
            self.stt("dve", t1, self.UE[:, j, 1:513], cw[:, j * 3 + 1: j * 3 + 2], t1, ALU.mult, ALU.add, uk + ["PRM", ("TMP", s)], [("TMP", s)])
            self.stt("dve", t1, self.UE[:, j, 0:512], cw[:, j * 3: j * 3 + 1], t1, ALU.mult, ALU.add, uk + ["PRM", ("TMP", s)], [("TMP", s)])
            self.tt("dve", self.AB[:, j, :], t1, self.PS[bs[0]][:, 0:T], ALU.mult, [("TMP", s), ("ps", bs[0])], [("AB", j)])
        self.mlin([(Win, KC, O_CB, hr)], 512, ev)

    def v_evac(self, l_unused, blk, src, skeys):
        for kv in range(2):
            self.cp("act", self.VT[:, blk, kv, 0, 0:64], src[:, kv * 64:(kv + 1) * 64], skeys, [("VT", blk)])
            self.cp("dve", self.VT[:, blk, kv, 1, 64:128], src[:, kv * 64:(kv + 1) * 64], skeys, [("VT", blk)])

    def attn_kv(self, l):
        Win = self.w_in[l]
        self.cp("pool", self.KT[:, :, 0:128], self.KCA[l][:, :, :], ["KCA%d" % l], ["KT"])
        self.cp("pool", self.VT[:, 0, :, :, :], self.VCA[l][:, :, :, :], ["VCA%d" % l], [("VT", 0)])
        slot = self.wload(Win, 0, KC, [(0, O_AK, 64), (64, O_AK, 64), (128, O_AK + 64, 64), (192, O_AK + 64, 64)], 256)
        wa = self.wap(slot, KC, 256)
        for kv in range(2):
            b = self.bank()
            for kc in range(KC):
                self.mm(self.PS[b][:, 0:T], wa[:, kc, kv * 128:(kv + 1) * 128], self.H[:, kc, :], kc == 0, kc == KC - 1,
                        [("WS", slot), ("H", kc)], [("ps", b)])
            self.cp("act", self.KT[:, kv, 128:640], self.PS[b][:, 0:T], [("ps", b)], ["KT"])
        slot = self.wload(Win, 0, KC, [(0, O_AV, 128)], 128)
        wa = self.wap(slot, KC, 128)
        b = self.bank()
        for tb in range(4):
            for kc in range(KC):
                self.mm(self.PS[b][:, tb * 128:(tb + 1) * 128], self.H[:, kc, tb * 128:(tb + 1) * 128], wa[:, kc, :],
                        kc == 0, kc == KC - 1, [("WS", slot), ("H", kc)], [("ps", b)])
        for tb in range(4):
            self.v_evac(l, 1 + tb, self.PS[b][:, tb * 128:(tb + 1) * 128], [("ps", b)])

    def attn_carry(self, l):
        self.cp("pool", self.KCA[l][:, :, :], self.KT[:, :, 512:640], ["KT"], ["KCA%d" % l])
        self.cp("pool", self.VCA[l][:, :, :, :], self.VT[:, 4, :, :, :], [("VT", 4)], ["VCA%d" % l])

    def attn(self, l, t):
        Win = self.w_in[l]
        hr = lambda kc: (self.H[:, kc, :], [("H", kc)])
        self.attn_kv(l)

        def evq(j, bs):
            self.act(self.AB[:, 32 + j, :], self.PS[bs[0]][:, 0:T], AF.Copy, [("ps", bs[0])], [("AB", 32 + j)], scale=0.125)
        self.mlin([(Win, KC, O_AQ, hr)], 1024, evq)
        sink = self.PRM[:, l * P_W + P_SINK: l * P_W + P_SINK + 16]
        for tb in range(4):
            dd = self.CST[:, C_D0:C_D0 + 256] if (t == 0 and tb == 0) else self.CST[:, C_DD:C_DD + 256]
            for c in range(8):
                bo = self.bank()
                for hp in range(2):
                    h = 2 * c + hp
                    kv = h // 8
                    slope = float(2.0 ** (-8.0 * (h + 1) / 16.0))
                    i = self.at_i % 2
                    self.at_i += 1
                    bs = self.bank()
                    self.mm(self.PS[bs][:, 0:256], self.AB[hp * 64:(hp + 1) * 64, 32 + c, tb * 128:(tb + 1) * 128],
                            self.KT[hp * 64:(hp + 1) * 64, kv, tb * 128: tb * 128 + 256], True, True,
                            [("AB", 32 + c), "KT"], [("ps", bs)])
                    sm = self.SM[:, i, :]
                    self.stt("dve", sm, dd, -slope, self.PS[bs][:, 0:256], ALU.mult, ALU.add, ["CST", ("ps", bs)], [("SM", i)])
                    st = self.AST[:, i, :]
                    self.op("dve", lambda e, st=st, sm=sm: e.reduce_max(out=st[:, 0:1], in_=sm, axis=AX.X), [("SM", i)], [("AST", i)])
                    self.ts("dve", st[:, 1:2], st[:, 0:1], sink[:, h:h + 1], -1.0, ALU.max, ALU.mult, [("AST", i), "PRM"], [("AST", i)])
                    pb = self.PB[:, i, :]
                    self.act(pb, sm, AF.Exp, [("SM", i), ("AST", i)], [("PB", i), ("AST", i)], bias=st[:, 1:2], accum=st[:, 2:3])
                    self.act(st[:, 3:4], sink[:, h:h + 1], AF.Exp, ["PRM", ("AST", i)], [("AST", i)], bias=st[:, 1:2])
                    self.tt("dve", st[:, 4:5], st[:, 2:3], st[:, 3:4], ALU.add, [("AST", i)], [("AST", i)])
                    self.op("dve", lambda e, st=st: e.reciprocal(out=st[:, 5:6], in_=st[:, 4:5]), [("AST", i)], [("AST", i)])
                    self.ts("pool", pb, pb, st[:, 5:6], None, ALU.mult, None, [("PB", i), ("AST", i)], [("PB", i)])
                    bt = self.bank()
                    ptv = self.PS[bt][:, :].bitcast(BF16)
                    for kb in range(2):
                        self.tr(ptv[:, kb * 128:(kb + 1) * 128], pb[:, kb * 128:(kb + 1) * 128], self.IDB[:, :],
                                [("PB", i), "IDB"], [("ps", bt)])
                    j = self.pt_i % 4
                    self.pt_i += 1
                    self.cp(self.ev_eng(), self.PTS[:, j, :], ptv[:, 0:256], [("ps", bt)], [("PTS", j)])
                    for kb in range(2):
                        self.mm(self.PS[bo][:, 0:128], self.VT[:, tb + kb, kv, hp, :], self.PTS[:, j, kb * 128:(kb + 1) * 128],
                                hp == 0 and kb == 0, hp == 1 and kb == 1, [("VT", tb + kb), ("PTS", j)], [("ps", bo)])
                self.cp(self.ev_eng(), self.AB[:, 4 + c, tb * 128:(tb + 1) * 128], self.PS[bo][:, 0:128], [("ps", bo)], [("AB", 4 + c)])
        self.attn_carry(l)

    def hgrn_itok(self, l):
        Win = self.w_in[l]
        for half in range(2):
            slot = self.wload(Win, 0, KC, [(0, O_RI + half * 256, 256)], 256)
            wa = self.wap(slot, KC, 256)
            for cg in range(4):
                b = self.bank()
                for cc in range(2):
                    ch = cg * 2 + cc
                    for kc in range(KC):
                        self.mm(self.PS[b][0:64, cc * 256:(cc + 1) * 256], self.H[:, kc, ch * 64:(ch + 1) * 64], wa[:, kc, :],
                                kc == 0, kc == KC - 1, [("WS", slot), ("H", kc)], [("ps", b)])
                self.cp(self.ev_eng(), self.ITOK[0:64, cg * 2:cg * 2 + 2, half * 256:(half + 1) * 256],
                        self.PS[b][0:64, 0:512].rearrange("p (c m) -> p c m", c=2), [("ps", b)], [("AB", 32 + k) for k in range(8)])

    def hgrn(self, l, full):
        Win = self.w_in[l]
        hr = lambda kc: (self.H[:, kc, :], [("H", kc)])
        self.hgrn_itok(l)
        itk = [("AB", 32 + k) for k in range(8)]
        lbb = self.LB[:, l, :, :]
        f0 = 2056
        Fsig = self.YF[:, f0 + 2048: f0 + 2560]
        Ff = self.YF[:, f0 + 2560: f0 + 3072]
        Fkc = self.YF[:, f0 + 3072: f0 + 3584]
        Fa = self.YF[:, f0 + 3584: f0 + 4096]
        Fea = self.YF[:, f0 + 4096: f0 + 4608]
        Fena = self.YF[:, f0 + 4608: f0 + 5120]
        Fsq = self.YF[:, f0 + 5120: f0 + 5632]
        fk = self.ykeys(f0 + 2048, f0 + 5632)
        Sst = self.ST[l]
        sk = "ST%d" % l
        OF = self.YF[:, f0: f0 + 2048].rearrange("p (q t) -> p q t", t=T)
        ofk = self.ykeys(f0, f0 + 2048)
        for hg in range(2):
            for hh in range(4):
                h = hg * 4 + hh
                QT, KTl = 16 + hh, 20 + hh
                KTT = self.AB[:, 24 + 2 * hh: 26 + 2 * hh, :].rearrange("p a t -> p (a t)")
                kttk = [("AB", 24 + 2 * hh), ("AB", 25 + 2 * hh)]

                def evz(j, bs):
                    self.act(Fsig, self.PS[bs[0]][:, 0:T], AF.Sigmoid, [("ps", bs[0])], fk)
                self.mlin([(Win, KC, O_RF + h * 128, hr)], 128, evz)
                self.ts("dve", Ff, Fsig, lbb[:, 1, h:h + 1], lbb[:, 0, h:h + 1], ALU.mult, ALU.add, fk + ["LB"], fk)
                self.act(Ff, Ff, AF.Ln, fk, fk)
                self.ts("pool", Fkc, Fsig, lbb[:, 3, h:h + 1], lbb[:, 2, h:h + 1], ALU.mult, ALU.add, fk + ["LB"], fk)
                self.op("dve", lambda e: e.tensor_tensor_scan(out=Fa, data0=self.CST[:, C_SM:C_SM + 512], data1=Ff, initial=0.0,
                                                              op0=ALU.mult, op1=ALU.add), fk + ["CST"], fk)
                a3 = Fa.rearrange("p (c t) -> p c t", t=64)
                self.act(self.E1[:, h, :], a3[:, :, 63], AF.Exp, fk, ["E1"])
                if full:
                    self.act(self.EM[:, h, :], a3[:, :, 31], AF.Exp, fk, ["EM"])
                else:
                    self.op("dve", lambda e, h=h: e.reduce_sum(out=self.ATT[:, h:h + 1], in_=a3[:, :, 63], axis=AX.X), fk, ["ATT"])
                self.cp("dve", self.AM[:, :], a3[:, :, 31], fk, ["AM"])
                self.tt("dve", a3, a3, self.AM[:, :].unsqueeze(2).to_broadcast([128, 8, 64]), ALU.subtract, fk + ["AM"], fk)
                self.act(Fena, Fa, AF.Exp, fk, fk, scale=-1.0)
                self.act(Fea, Fa, AF.Exp, fk, fk)
                self.cp("pool", self.E2[:, h, :], Fea.rearrange("p (c t) -> p c t", t=64)[:, :, 63], fk, ["E2"])
                self.tt("dve", self.AB[:, KTl, :], Fkc, Fena, ALU.mult, fk, [("AB", KTl)])
                if full:
                    def evq(j, bs):
                        self.act(Fsq, self.PS[bs[0]][:, 0:T], AF.Silu, [("ps", bs[0])], fk)
                    self.mlin([(Win, KC, O_RQ + h * 128, hr)], 128, evq)
                    self.tt("dve", self.AB[:, QT, :], Fsq, Fea, ALU.mult, fk, [("AB", QT)])
                for g in range(2):
                    bt = self.bank()
                    ptv = self.PS[bt][:, :].bitcast(BF16)
                    for cc in range(4):
                        ch = g * 4 + cc
                        self.tr(ptv[0:64, cc * 128:(cc + 1) * 128], self.AB[:, KTl, ch * 64:(ch + 1) * 64], self.IDB[:, :],
                                [("AB", KTl), "IDB"], [("ps", bt)])
                    self.cp(self.ev_eng(), KTT[0:64, g * 512:(g + 1) * 512], ptv[0:64, 0:512], [("ps", bt)], kttk)
            h0 = hg * 4
            for ch in range(8):
                if full:
                    self.tt("dve", self.SBF[:, h0:h0 + 4, :], Sst[:, h0:h0 + 4, :],
                            self.EM[:, h0:h0 + 4, ch:ch + 1].to_broadcast([128, 4, 64]), ALU.mult, [sk, "EM"], ["SBF"])
                    ba = self.bank()
                    for hh in range(4):
                        self.mm(self.PS[ba][0:64, hh * 64:(hh + 1) * 64], self.AB[:, 20 + hh, ch * 64:(ch + 1) * 64],
                                self.AB[:, 16 + hh, ch * 64:(ch + 1) * 64], True, True, [("AB", 20 + hh), ("AB", 16 + hh)], [("ps", ba)])
                    self.tt("dve", self.ATS[0:64, 0:256], self.PS[ba][0:64, 0:256], self.CST[0:64, C_CM:C_CM + 256], ALU.mult,
                            [("ps", ba), "CST"], ["ATS"])
                    bo = self.bank()
                    for hh in range(4):
                        h = h0 + hh
                        q, hp = hh // 2, hh % 2
                        o = self.PS[bo][hp * 64:(hp + 1) * 64, q * 64:(q + 1) * 64]
                        self.mm(o, self.SBF[:, h, :], self.AB[:, 16 + hh, ch * 64:(ch + 1) * 64], True, False,
                                ["SBF", ("AB", 16 + hh)], [("ps", bo)])
                        self.mm(o, self.ITOK[0:64, ch, h * 64:(h + 1) * 64], self.ATS[0:64, hh * 64:(hh + 1) * 64], False, True,
                                itk + ["ATS"], [("ps", bo)])
                    self.cp("act", OF[:, 2 * hg:2 * hg + 2, ch * 64:(ch + 1) * 64],
                            self.PS[bo][:, 0:128].rearrange("p (q t) -> p q t", t=64), [("ps", bo)], ofk)
                bm = self.bank()
                for hh in range(4):
                    h = h0 + hh
                    KTT = self.AB[:, 24 + 2 * hh: 26 + 2 * hh, :].rearrange("p a t -> p (a t)")
                    self.mm(self.PS[bm][:, hh * 64:(hh + 1) * 64], KTT[0:64, ch * 128:(ch + 1) * 128],
                            self.ITOK[0:64, ch, h * 64:(h + 1) * 64], True, True,
                            [("AB", 24 + 2 * hh), ("AB", 25 + 2 * hh)] + itk, [("ps", bm)])
                pm = self.PS[bm][:, 0:256].rearrange("p (h v) -> p h v", v=64)
                self.tt("dve", self.SBT[:, 0:4, :], pm, self.E2[:, h0:h0 + 4, ch:ch + 1].to_broadcast([128, 4, 64]), ALU.mult,
                        [("ps", bm), "E2"], ["SBT"])
                self.tt("pool", Sst[:, h0:h0 + 4, :], Sst[:, h0:h0 + 4, :], self.E1[:, h0:h0 + 4, ch:ch + 1].to_broadcast([128, 4, 64]),
                        ALU.mult, [sk, "E1"], [sk])
                self.tt("dve", Sst[:, h0:h0 + 4, :], Sst[:, h0:h0 + 4, :], self.SBT[:, 0:4, :], ALU.add, [sk, "SBT"], [sk])
        if not full:
            self.tt("dve", self.AT[l][:, :], self.AT[l][:, :], self.ATT[:, :], ALU.add, ["ATT", "AT%d" % l], ["AT%d" % l])
            return
        hn = self.PRM[:, l * P_W + P_HN: l * P_W + P_HN + 1]

        def evg(q, bs):
            s = self.tmp_i % 4
            self.tmp_i += 1
            sq = self.TMP[:, s, :]
            self.act(sq, OF[:, q, :], AF.Square, ofk, [("TMP", s)])
            b = self.bank()
            self.mm(self.PS[b][:, 0:T], self.CST[:, C_BD:C_BD + 128], sq, True, True, [("TMP", s), "CST"], [("ps", b)])
            self.act(sq, self.PS[b][:, 0:T], AF.Sqrt, [("ps", b), "EPSC"], [("TMP", s)], bias=self.EPSC[:, 0:1])
            self.op("dve", lambda e: e.reciprocal(out=sq, in_=sq), [("TMP", s)], [("TMP", s)])
            self.stt("dve", sq, OF[:, q, :], hn[:, 0:1], sq, ALU.mult, ALU.mult, ofk + ["PRM", ("TMP", s)], [("TMP", s)])
            s2 = self.tmp_i % 4
            self.tmp_i += 1
            gs = self.TMP[:, s2, :]
            self.act(gs, self.PS[bs[0]][:, 0:T], AF.Silu, [("ps", bs[0])], [("TMP", s2)])
            self.tt("dve", self.AB[:, 12 + q, :], sq, gs, ALU.mult, [("TMP", s), ("TMP", s2)], [("AB", 12 + q)])
        self.mlin([(Win, KC, O_RG, hr)], 512, evg)

    def merge(self, l, use):
        Win = self.w_in[l]
        hr = lambda kc: (self.H[:, kc, :], [("H", kc)])
        stages = [("conv", self.w_bc[l], 4, 0, O_GC), ("attn", self.w_ba[l], 8, 4, O_GA), ("hgrn", self.w_bh[l], 4, 12, O_GH)]
        stages = [s for s in stages if s[0] in use]
        for si, (nm, Wb, kch, ab0, og) in enumerate(stages):
            first = si == 0
            last = si == len(stages) - 1

            def ev(m, bs, first=first, last=last):
                s = self.tmp_i % 4
                self.tmp_i += 1
                tm = self.TMP[:, s, :]
                self.act(tm, self.PS[bs[0]][:, 0:T], AF.Sigmoid, [("ps", bs[0])], [("TMP", s)])
                if first and last:
                    self.tt("dve", self.AB[:, 16 + m, :], tm, self.PS[bs[1]][:, 0:T], ALU.mult, [("TMP", s), ("ps", bs[1])], [("AB", 16 + m)])
                elif first:
                    self.tt("dve", self.Y[:, m, :], tm, self.PS[bs[1]][:, 0:T], ALU.mult, [("TMP", s), ("ps", bs[1])], [("Y", m)])
                else:
                    self.tt("dve", tm, tm, self.PS[bs[1]][:, 0:T], ALU.mult, [("TMP", s), ("ps", bs[1])], [("TMP", s)])
                    if last:
                        self.tt("pool", self.AB[:, 16 + m, :], tm, self.Y[:, m, :], ALU.add, [("TMP", s), ("Y", m)], [("AB", 16 + m)])
                    else:
                        self.tt("pool", self.Y[:, m, :], tm, self.Y[:, m, :], ALU.add, [("TMP", s), ("Y", m)], [("Y", m)])
            self.mlin([(Win, KC, og, hr), (Wb, kch, 0, lambda kc, ab0=ab0: (self.AB[:, ab0 + kc, :], [("AB", ab0 + kc)]))], D, ev)

        def evo(m, bs):
            self.cp(self.ev_eng(), self.Y[:, m, :], self.PS[bs[0]][:, 0:T], [("ps", bs[0])], [("Y", m)])
        self.mlin([(self.w_o[l], KC, 0, lambda kc: (self.AB[:, 16 + kc, :], [("AB", 16 + kc)]))], D, evo)
        self.postnorm_add(l, G_MPOST, False)

    def part_a(self, l, t):
        if "ffn1" in self.parts:
            self.ffn(l, self.w_gu1, self.w_d1, G_F1PRE, G_F1POST)
        mix = self.parts & {"conv", "attn", "hgrn"}
        if not mix:
            return
        need_halo = (t == self.nt - 1) and (mix & {"conv", "attn"})
        if "hgrn" in mix or need_halo:
            self.prenorm(l, G_MPRE)
        if "hgrn" in mix:
            self.hgrn(l, False)
        if t == self.nt - 1:
            if "conv" in mix:
                self.conv_u(l)
                self.conv_carry(l)
            if "attn" in mix:
                self.attn_kv(l)
                self.attn_carry(l)

    def part_b(self, l, t):
        mix = self.parts & {"conv", "attn", "hgrn"}
        if mix:
            self.prenorm(l, G_MPRE)
            if "conv" in mix:
                self.conv_u(l)
                self.conv_out(l)
                self.conv_carry(l)
            if "attn" in mix:
                self.attn(l, t)
            if "hgrn" in mix:
                self.hgrn(l, True)
            self.merge(l, mix)
        if "ffn2" in self.parts:
            self.ffn(l, self.w_gu2, self.w_d2, G_F2PRE, G_F2POST)
        if "ple" in self.parts:
            self.ple(l, t)

    def load_x_tok(self, t):
        st = self.YF[:, :].rearrange("p (b f) -> p b f", b=4)
        for tb in range(4):
            self.dma("sp", st[:, tb, :], self.x_d[t * T + tb * 128: t * T + (tb + 1) * 128, :], [],
                     [("Y", 4 * tb + k) for k in range(4)], "xld")
        for c in range(KC):
            b = self.bank()
            for tb in range(4):
                self.tr(self.PS[b][:, tb * 128:(tb + 1) * 128], st[:, tb, c * 128:(c + 1) * 128], self.IDF[:, :],
                        [("Y", 4 * tb + k) for k in range(4)] + ["CST"], [("ps", b)])
            self.cp(self.ev_eng(), self.X[:, c, :], self.PS[b][:, 0:T], [("ps", b)], [("X", c)])

    def store_out(self, t):
        st = self.YF[:, :].rearrange("p (b f) -> p b f", b=4)
        for tb in range(4):
            for cg in range(4):
                b = self.bank()
                for cc in range(4):
                    c = cg * 4 + cc
                    self.tr(self.PS[b][:, cc * 128:(cc + 1) * 128], self.X[:, c, tb * 128:(tb + 1) * 128], self.IDF[:, :],
                            [("X", c), "CST"], [("ps", b)])
                self.cp(self.ev_eng(), st[:, tb, cg * 512:(cg + 1) * 512], self.PS[b][:, 0:512], [("ps", b)],
                        [("Y", 4 * tb + k) for k in range(4)])
            self.dma("sp", self.out_d[t * T + tb * 128: t * T + (tb + 1) * 128, :], st[:, tb, :],
                     [("Y", 4 * tb + k) for k in range(4)], [], "ost")

    def xscr(self, i, t):
        return self.xs_d[i][:, :, t * T:(t + 1) * T]

    def exchange(self, l):
        nr = self.ncores
        EXk = self.ykeys(0, nr * NX)
        EX = self.YF[:, 0:nr * NX].rearrange("p (r f) -> p r f", f=NX)
        sk = "ST%d" % l
        tf = self.TMP[:, :, :].rearrange("p a t -> p (a t)")
        snd = tf[:, 0:NX]
        sndk = [("TMP", 0), ("TMP", 1)]
        self.cp("dve", snd[:, 0:512], self.ST[l][:, :, :].rearrange("p h v -> p (h v)"), [sk], sndk)
        self.act(snd[:, 512:520], self.AT[l][:, :], AF.Exp, ["AT%d" % l], sndk)
        self.cp("dve", snd[:, 520:776], self.KCA[l][:, :, :].rearrange("p a b -> p (a b)"), ["KCA%d" % l], sndk)
        for kv in range(2):
            self.cp("dve", snd[:, 776 + kv * 64: 776 + (kv + 1) * 64], self.VCA[l][:, kv, 0, 0:64], ["VCA%d" % l], sndk)
        self.cp("dve", snd[:, 904:912], self.UC[l][:, :, :].rearrange("p a b -> p (a b)"), ["UC%d" % l], sndk)
        self.dma("sp", self.agin[l], snd, sndk, ["agin%d" % l], "agi")
        self.op("pool", lambda e: e.collective_compute("AllGather", ALU.bypass, replica_groups=[list(range(self.ncores))],
                                                       ins=[self.agin[l]], outs=[self.agout[l]]),
                ["agin%d" % l], ["agout%d" % l], dma="cc", inc=1)
        self.dma("sp", EX, self.agout[l].rearrange("(r p) f -> p r f", p=128), ["agout%d" % l], EXk, "ago")
        cm = self.CST[:, C_M:C_M + 8]
        cm1 = self.CST[:, C_M1:C_M1 + 8]
        cs = self.CST[:, C_S:C_S + 8]
        S3 = self.ST[l]
        self.op("pool", lambda e: e.memset(S3[:, :, :], 0.0), [], [sk])
        for r in range(self.ncores):
            self.ts("dve", self.DR[:, :], EX[:, r, 512:520], cm[:, r:r + 1], cm1[:, r:r + 1], ALU.mult, ALU.add, EXk + ["CST"], ["DR"])
            self.tt("dve", S3[:, :, :], S3[:, :, :], self.DR[:, :].unsqueeze(2).to_broadcast([128, 8, 64]), ALU.mult, [sk, "DR"], [sk])
            self.stt("dve", S3[:, :, :], EX[:, r, 0:512].rearrange("p (h v) -> p h v", v=64), cm[:, r:r + 1], S3[:, :, :],
                     ALU.mult, ALU.add, EXk + [sk, "CST"], [sk])
        hk = [("TMP", 2)]
        HA = tf[:, 1024:1416]
        for r in range(self.ncores):
            if r == 0:
                self.ts("dve", HA, EX[:, r, 520:912], cs[:, r:r + 1], None, ALU.mult, None, EXk + ["CST"], hk)
            else:
                self.stt("dve", HA, EX[:, r, 520:912], cs[:, r:r + 1], HA, ALU.mult, ALU.add, EXk + ["CST"] + hk, hk)
        self.cp("dve", self.KCA[l][:, :, :], HA[:, 0:256].rearrange("p (a b) -> p a b", a=2), hk, ["KCA%d" % l])
        for kv in range(2):
            self.cp("dve", self.VCA[l][:, kv, 0, 0:64], HA[:, 256 + kv * 64: 256 + (kv + 1) * 64], hk, ["VCA%d" % l])
            self.cp("dve", self.VCA[l][:, kv, 1, 64:128], HA[:, 256 + kv * 64: 256 + (kv + 1) * 64], hk, ["VCA%d" % l])
        self.cp("dve", self.UC[l][:, :, :], HA[:, 384:392].rearrange("p (a b) -> p a b", a=4), hk, ["UC%d" % l])

    def convert(self, n, l):
        k, m = WSHAPES[n]
        rows = max(128, min(k, (12 << 20) // (m * 4) // 128 * 128))
        for r0 in range(0, k, rows):
            r1 = min(k, r0 + rows)
            self.dma("pool", self.wbf[n][l][r0:r1, :], self.wsrc[n][l, r0:r1, :], [], [("WB", n, l)], "cv_%s_%d" % (n, l))

    def setup(self):
        self.dma("sp", self.CST[:, :], self.cst_d, [], ["CST"], "cld")
        self.dma("sp", self.PRM[:, :], self.prm_d, [], ["PRM"], "cld")
        self.cp("dve", self.IDB[:, :], self.IDF[:, :], ["CST"], ["IDB"])
        self.op("pool", lambda e: e.memset(self.ONES[:, :], 1.0 / D), [], ["ONES"])
        self.op("pool", lambda e: e.memset(self.EPSC[:, :], EPS), [], ["EPSC"])
        self.ts("dve", self.GH[:, :], self.PRM[:, :], 0.5, None, ALU.mult, None, ["PRM"], ["GH"])
        self.op("pool", lambda e: e.memset(self.VT[:, :, :, :, :], 0.0), [], [("VT", b) for b in range(5)])
        for l in range(self.depth):
            self.op("pool", lambda e, l=l: e.memset(self.ST[l][:, :, :], 0.0), [], ["ST%d" % l])
            self.op("pool", lambda e, l=l: e.memset(self.AT[l][:, :], 0.0), [], ["AT%d" % l])
            self.op("pool", lambda e, l=l: e.memset(self.UC[l][:, :, :], 0.0), [], ["UC%d" % l])
            self.op("pool", lambda e, l=l: e.memset(self.KCA[l][:, :, :], 0.0), [], ["KCA%d" % l])
            self.op("pool", lambda e, l=l: e.memset(self.VCA[l][:, :, :, :], 0.0), [], ["VCA%d" % l])
        L = self.depth
        lg = [self.PRM[:, l * P_W + P_LB: l * P_W + P_LB + 8] for l in range(L)]
        W = self.LBW
        self.cp("dve", W[:, 0, :], lg[0], ["PRM"], ["LBW"])
        for l in range(1, L):
            self.tt("dve", W[:, 0, :], W[:, 0, :], lg[l], ALU.max, ["PRM", "LBW"], ["LBW"])
        for l in range(L):
            self.tt("dve", W[:, 3 + l, :], lg[l], W[:, 0, :], ALU.subtract, ["PRM", "LBW"], ["LBW"])
            self.act(W[:, 3 + l, :], W[:, 3 + l, :], AF.Exp, ["LBW"], ["LBW"])
            if l == 0:
                self.cp("dve", W[:, 1, :], W[:, 3, :], ["LBW"], ["LBW"])
            else:
                self.tt("dve", W[:, 1, :], W[:, 1, :], W[:, 3 + l, :], ALU.add, ["LBW"], ["LBW"])
        self.op("dve", lambda e: e.reciprocal(out=W[:, 1, :], in_=W[:, 1, :]), ["LBW"], ["LBW"])
        self.op("pool", lambda e: e.memset(W[:, 2, :], 0.0), ["LBW"], ["LBW"])
        for l in range(L):
            lb = self.LB[:, l, :, :]
            self.ts("dve", lb[:, 0, :], W[:, 2, :], 1e-20, None, ALU.max, None, ["LBW"], ["LB"])
            self.ts("dve", lb[:, 1, :], lb[:, 0, :], -1.0, 1.0, ALU.mult, ALU.add, ["LB"], ["LB"])
            self.ts("dve", lb[:, 2, :], W[:, 2, :], -1.0, 1.0, ALU.mult, ALU.add, ["LBW"], ["LB"])
            self.ts("dve", lb[:, 3, :], lb[:, 2, :], -1.0, None, ALU.mult, None, ["LB"], ["LB"])
            self.tt("dve", W[:, 3 + l, :], W[:, 3 + l, :], W[:, 1, :], ALU.mult, ["LBW"], ["LBW"])
            self.tt("dve", W[:, 2, :], W[:, 2, :], W[:, 3 + l, :], ALU.add, ["LBW"], ["LBW"])

    def program(self):
        self.bank_i = 0
        self.ev_i = 0
        self.sq_i = 0
        self.tmp_i = 0
        self.at_i = 0
        self.pt_i = 0
        self.wi = 0
        self.wl = 0
        self.setup()
        L = self.depth
        for l in range(L):
            for grp in (WORDER_A, WORDER_B):
                for n in grp:
                    self.convert(n, l)
        mixx = bool(self.parts & {"conv", "attn", "hgrn"})
        for t in range(self.nt):
            self.load_x_tok(t)
            self.part_a(0, t)
            self.dma("sp", self.xscr(0, t), self.X[:, :, :], [("X", c) for c in range(KC)], ["xs0"], "xst")
        for l in range(L):
            if mixx:
                self.exchange(l)
            for t in range(self.nt):
                self.dma("sp", self.X[:, :, :], self.xscr(l % 2, t), ["xs%d" % (l % 2)], [("X", c) for c in range(KC)], "xrl")
                self.part_b(l, t)
                if l + 1 < L:
                    self.part_a(l + 1, t)
                    self.dma("sp", self.xscr((l + 1) % 2, t), self.X[:, :, :], [("X", c) for c in range(KC)], ["xs%d" % ((l + 1) % 2)], "xst")
                else:
                    self.store_out(t)

    def build(self, ncores):
        self.ncores = ncores
        nc = bass.Bass("TRN2", target_bir_lowering=False)
        self.nc = nc
        L, seg = self.depth, self.seg

        def din(name, shape):
            return nc.dram_tensor(name, shape, F32, kind="ExternalInput").ap()
        self.x_d = din("x", [seg, D])
        self.p_d = din("p", [L, seg, PLE])
        self.wsrc = {n: din(n, [L, k, m]) for n, (k, m) in WSHAPES.items()}
        self.wbf = {n: [nc.dram_tensor("bf_%s_%d" % (n, l), [k, m], BF16, kind="Internal").ap() for l in range(L)]
                    for n, (k, m) in WSHAPES.items()}
        self.w_gu1, self.w_d1, self.w_in = WRef("ffn1_w_gu"), WRef("ffn1_w_down"), WRef("w_in")
        self.w_bc, self.w_ba, self.w_bh = WRef("w_branch_conv"), WRef("w_branch_attn"), WRef("w_branch_hgrn")
        self.w_o, self.w_gu2, self.w_d2 = WRef("w_o"), WRef("ffn2_w_gu"), WRef("ffn2_w_down")
        self.w_ple_gate, self.w_ple_proj = WRef("w_ple_gate"), WRef("w_ple_proj")
        self.prm_d = din("prm", [128, L * P_W])
        self.cst_d = din("cst", [128, C_W])
        self.out_d = nc.dram_tensor("out", [seg, D], F32, kind="ExternalOutput").ap()
        self.xs_d = [nc.dram_tensor("xs%d" % i, [128, KC, seg], F32, kind="Internal").ap() for i in range(2)]
        self.agin = [nc.dram_tensor("agin%d" % l, [128, NX], F32, kind="Internal").ap() for l in range(L)]
        self.agout = [nc.dram_tensor("agout%d" % l, [ncores * 128, NX], F32, kind="Internal").ap() for l in range(L)]
        self.dry = True
        self.alloc_none()
        self.program()
        self.dry = False
        self.S = Sched()
        with ExitStack() as es:
            def sb(name, shape, dt):
                return es.enter_context(nc.sbuf_tensor(name, shape, dt))
            self.alloc(sb, es)
            self.program()
            names = ["e_" + e for e in Sched.ENGS] + ["d_" + str(k) for k in self.S.dma_keys]
            sems = {nm: es.enter_context(nc.semaphore(nm)) for nm in names}
            with nc.Block() as block:
                self.S.emit(block, sems)
        return nc

    def alloc_none(self):
        class Fake:
            def __getitem__(self, k):
                return self

            def __getattr__(self, k):
                return lambda *a, **kw: self
        f = Fake()
        for nm in ("X", "H", "Y", "YF", "AB", "ABF", "TMP", "RS", "CST", "PRM", "GH", "IDF", "IDB", "ONES", "EPSC", "KT", "VT",
                   "ITOK", "E1", "E2", "EM", "ATT", "SBF", "SBT", "ATS", "SM", "AST", "PB", "PTS", "PT", "UE", "LB", "LBW", "DR", "AM"):
            setattr(self, nm, f)
        self.WS = [f] * NSLOT
        self.PS = [f] * 8
        self.ST = [f] * self.depth
        self.AT = [f] * self.depth
        self.UC = [f] * self.depth
        self.KCA = [f] * self.depth
        self.VCA = [f] * self.depth

    def alloc(self, sb, es):
        nc = self.nc
        L = self.depth
        self.X = sb("X", [128, KC, T], F32)
        self.H = sb("H", [128, KC, T], BF16)
        self.YF = sb("YF", [128, KC * T], F32)
        self.Y = self.YF[:, :].rearrange("p (c t) -> p c t", t=T)
        self.AB = sb("AB", [128, FC, T], BF16)
        self.ABF = self.AB[:, 0:8, :].rearrange("p a t -> p (a t)").bitcast(F32).rearrange("p (s t) -> p s t", t=T)
        self.ITOK = self.AB[:, 32:40, :].rearrange("p a t -> p (a t)").rearrange("p (c m) -> p c m", m=512)
        self.TMP = sb("TMP", [128, 4, T], F32)
        self.RS = sb("RS", [128, T], F32)
        self.CST = sb("CST", [128, C_W], F32)
        self.IDF = self.CST[:, C_ID:C_ID + 128]
        self.PRM = sb("PRM", [128, L * P_W], F32)
        self.GH = sb("GH", [128, L * P_W], F32)
        self.IDB = sb("IDB", [128, 128], BF16)
        self.ONES = sb("ONES", [128, 128], F32)
        self.EPSC = sb("EPSC", [128, 1], F32)
        self.KT = sb("KT", [128, 2, 640], BF16)
        self.VT = sb("VT", [128, 5, 2, 2, 128], BF16)
        self.E1 = sb("E1", [128, 8, 8], F32)
        self.E2 = sb("E2", [128, 8, 8], F32)
        self.EM = sb("EM", [128, 8, 8], F32)
        self.ATT = sb("ATT", [128, 8], F32)
        self.SBF = sb("SBF", [128, 8, 64], BF16)
        self.SBT = sb("SBT", [128, 8, 64], F32)
        self.ATS = sb("ATS", [128, 512], BF16)
        self.SM = sb("SM", [128, 2, 256], F32)
        self.AST = sb("AST", [128, 2, 8], F32)
        self.PB = sb("PB", [128, 2, 256], BF16)
        self.PTS = sb("PTS", [128, 4, 256], BF16)
        self.PT = sb("PT", [128, 2, T], BF16)
        self.UE = self.YF[:, 0:2056].rearrange("p (j t) -> p j t", t=514)
        self.LB = sb("LB", [128, L, 4, 8], F32)
        self.LBW = sb("LBW", [128, 3 + L, 8], F32)
        self.DR = sb("DR", [128, 8], F32)
        self.AM = sb("AM", [128, 8], F32)
        self.WS = [sb("WS%d" % i, [128, SLOT_ELEMS], BF16) for i in range(NSLOT)]
        self.ST = [sb("ST%d" % l, [128, 8, 64], F32) for l in range(L)]
        self.AT = [sb("AT%d" % l, [128, 8], F32) for l in range(L)]
        self.UC = [sb("UC%d" % l, [128, 4, 2], F32) for l in range(L)]
        self.KCA = [sb("KCA%d" % l, [128, 2, 128], BF16) for l in range(L)]
        self.VCA = [sb("VCA%d" % l, [128, 2, 2, 128], BF16) for l in range(L)]
        self.PS = [es.enter_context(nc.psum_tensor("PS%d" % i, [128, 512], F32)) for i in range(8)]


ALL_PARTS = frozenset({"ffn1", "conv", "attn", "hgrn", "ffn2", "ple"})


def host_consts(nbatch, nseg):
    ncores = nbatch * nseg
    out = []
    ident = np.eye(128, dtype=np.float32)
    bd = np.zeros((128, 128), np.float32)
    bd[:64, :64] = 1.0 / 64
    bd[64:, 64:] = 1.0 / 64
    qi = np.arange(128)[:, None]
    ki = np.arange(256)[None, :]
    dist = (qi + 128 - ki).astype(np.float32)
    band = (dist >= 0) & (dist < 128)
    dd = np.where(band, dist, BIG).astype(np.float32)
    cm = np.zeros((128, 512), np.float32)
    s = np.arange(64)[:, None]
    tt = np.arange(64)[None, :]
    cm[:64] = np.tile((s <= tt).astype(np.float32), (1, 8))
    sm = np.ones((128, 512), np.float32)
    sm[:, ::64] = 0.0
    for c in range(ncores):
        b, sg = divmod(c, nseg)
        d0 = dd.copy()
        if sg == 0:
            d0[:, :128] = BIG
        m = np.zeros(8, np.float32)
        sel = np.zeros(8, np.float32)
        for r in range(ncores):
            rb, rs = divmod(r, nseg)
            if rb == b and rs < sg:
                m[r] = 1.0
            if rb == b and rs == sg - 1:
                sel[r] = 1.0
        cst = np.zeros((128, C_W), np.float32)
        cst[:, C_ID:C_ID + 128] = ident
        cst[:, C_BD:C_BD + 128] = bd
        cst[:, C_DD:C_DD + 256] = dd
        cst[:, C_D0:C_D0 + 256] = d0
        cst[:, C_CM:C_CM + 512] = cm
        cst[:, C_SM:C_SM + 512] = sm
        cst[:, C_M:C_M + 8] = m[None]
        cst[:, C_M1:C_M1 + 8] = (1.0 - m)[None]
        cst[:, C_S:C_S + 8] = sel[None]
        out.append(cst)
    return out


def host_params(inp, L):
    prm = np.zeros((128, L * P_W), np.float32)
    gn = ["ffn1_norm_pre", "ffn1_norm_post", "mix_norm_pre", "mix_norm_post", "ffn2_norm_pre", "ffn2_norm_post",
          "ple_norm_pre", "ple_norm_post"]
    for l in range(L):
        o = l * P_W
        for g, nm in enumerate(gn):
            prm[:, o + g * 16: o + (g + 1) * 16] = np.asarray(inp[nm][l], np.float32).reshape(16, 128).T
        cw = np.asarray(inp["conv_w"][l], np.float32)
        prm[:, o + P_CONV: o + P_CONV + 12] = cw.reshape(3, 4, 128).transpose(2, 1, 0).reshape(128, 12)
        prm[:, o + P_SINK: o + P_SINK + 16] = np.asarray(inp["attn_sinks"][l], np.float32)[None, :]
        prm[:, o + P_LB: o + P_LB + 8] = np.asarray(inp["hgrn_lb_logits"][l], np.float32).reshape(8, 128).T
        prm[:, o + P_HN] = np.tile(np.asarray(inp["hgrn_norm"][l], np.float32), 2)
    return prm


_CACHE = {}


def run(inputs, nbatch, nseg, seg, depth, parts=ALL_PARTS, trace=False):
    ncores = nbatch * nseg
    key = (ncores, seg, depth, parts)
    if key not in _CACHE:
        _CACHE[key] = Gen(seg, depth, frozenset(parts)).build(ncores)
    nc = _CACHE[key]
    csts = host_consts(nbatch, nseg)
    prm = host_params(inputs, depth)
    wnames = ["ffn1_w_gu", "ffn1_w_down", "w_in", "w_branch_conv", "w_branch_attn", "w_branch_hgrn", "w_o",
              "ffn2_w_gu", "ffn2_w_down", "w_ple_gate", "w_ple_proj"]
    ws = {n: np.ascontiguousarray(np.asarray(inputs[n], np.float32)) for n in wnames}
    x = np.asarray(inputs["x"], np.float32)
    p = np.asarray(inputs["p"], np.float32)
    maps = []
    for c in range(ncores):
        b, sg = divmod(c, nseg)
        m = dict(ws)
        m["x"] = np.ascontiguousarray(x[b, sg * seg:(sg + 1) * seg])
        m["p"] = np.ascontiguousarray(p[:, b, sg * seg:(sg + 1) * seg])
        m["prm"] = prm
        m["cst"] = csts[c]
        maps.append(m)
    res = run_bass_kernel_spmd(nc, maps, core_ids=list(range(ncores)), **({"trace": True} if trace else {}))
    out = np.zeros((nbatch, nseg * seg, D), np.float32)
    for c in range(ncores):
        b, sg = divmod(c, nseg)
        out[b, sg * seg:(sg + 1) * seg] = res.results[c]["out"]
    return out, res


def kernel(**inputs):
    out, _ = run(inputs, 2, 4, 4096, 2)
    return out
```
